# Optimizing a Trainium2 kernel written in Bass

```python
import jax
import jax.numpy as jnp
from jax import lax
import numpy as np

D_MODEL = 1024
BATCH = 2
SEQ = 16384
DEPTH = 2

GRID_W = 64
CTX_LEN = 256
HEAD_DIM = 64
ROPE_BASE = 10000.0
NEG_INF = -1e30
EPS = 1e-6

POOL_WINDOWS = (2, 4, 8, 16)
POOL_GROUPS = 4
POOL_GROUP_DIM = 64
POOL_W = POOL_GROUPS * POOL_GROUP_DIM
NA_HEADS = 4
NA_ROWS = 8
NA_COLS = 16
NA_W = NA_HEADS * HEAD_DIM
SWA_HEADS = 4
SWA_KV_HEADS = 2
SWA_WINDOW = 128
SWA_BLOCK = 128
MLA_HEADS = 4
MLA_Q_RANK = 256
MLA_KV_RANK = 128
MLA_NOPE = 64
MLA_ROPE = 32
MLA_V = 64
MLA_Q_BLOCK = 128
MLA_SCALE = (MLA_NOPE + MLA_ROPE) ** -0.5
N_BRANCH = 4
BRANCH_W = 256
FFN_HIDDEN = -(-8 * D_MODEL // (3 * 256)) * 256

OFF_NA_K = 0
OFF_NA_V = OFF_NA_K + NA_W
OFF_SWA_K = OFF_NA_V + NA_W
OFF_SWA_V = OFF_SWA_K + SWA_KV_HEADS * HEAD_DIM
OFF_MLA_CKV = OFF_SWA_V + SWA_KV_HEADS * HEAD_DIM
OFF_MLA_KR = OFF_MLA_CKV + MLA_KV_RANK
KV_COLS = OFF_MLA_KR + MLA_ROPE
OFF_NA_Q = KV_COLS
OFF_SWA_Q = OFF_NA_Q + NA_W
OFF_MLA_CQ = OFF_SWA_Q + SWA_HEADS * HEAD_DIM
OFF_POOL = OFF_MLA_CQ + MLA_Q_RANK
OFF_GATE = OFF_POOL + POOL_W
IN_COLS = OFF_GATE + N_BRANCH * D_MODEL

kernel_name = "hybrid_pool_natten_swa_mla_prefix_dit"

F32 = jnp.float32


def rms_norm(x, g):
    xf = x.astype(F32)
    y = xf * lax.rsqrt(jnp.mean(xf * xf, axis=-1, keepdims=True) + EPS)
    return (y * g.astype(F32)).astype(x.dtype)


def modulation(cond, w, b, k):
    m = jax.nn.silu(cond) @ w[:, :k * D_MODEL] + b[:k * D_MODEL]
    return jnp.split(m, k, axis=-1)


def modulate(x, g, shift, scale):
    return rms_norm(x, g) * (1 + scale[..., None, :]) + shift[..., None, :]


def axial_rope(n, dim):
    t = jnp.arange(n, dtype=jnp.int32)
    row = (t // GRID_W).astype(F32)
    col = (t % GRID_W).astype(F32)
    n_freq = dim // 4
    inv = jnp.power(ROPE_BASE, -jnp.arange(n_freq, dtype=F32) / n_freq)
    ang = jnp.concatenate([row[:, None] * inv, col[:, None] * inv], axis=-1)
    return jnp.cos(ang), jnp.sin(ang)


def apply_rope(x, cos, sin):
    half = x.shape[-1] // 2
    xf = x.astype(F32)
    x1, x2 = xf[..., :half], xf[..., half:]
    return jnp.concatenate([x1 * cos - x2 * sin, x2 * cos + x1 * sin], axis=-1).astype(x.dtype)


def kv_heads(u, kv_norm_g, rope):
    B, n, _ = u.shape
    na_k = u[..., OFF_NA_K:OFF_NA_V].reshape(B, n, NA_HEADS, HEAD_DIM)
    na_v = u[..., OFF_NA_V:OFF_SWA_K].reshape(B, n, NA_HEADS, HEAD_DIM)
    swa_k = u[..., OFF_SWA_K:OFF_SWA_V].reshape(B, n, SWA_KV_HEADS, HEAD_DIM)
    swa_v = u[..., OFF_SWA_V:OFF_MLA_CKV].reshape(B, n, SWA_KV_HEADS, HEAD_DIM)
    ckv = rms_norm(u[..., OFF_MLA_CKV:OFF_MLA_KR], kv_norm_g)
    kr = u[..., OFF_MLA_KR:KV_COLS]
    if rope is not None:
        cos_h, sin_h, cos_r, sin_r = rope
        swa_k = apply_rope(swa_k, cos_h[:, None], sin_h[:, None])
        kr = apply_rope(kr, cos_r, sin_r)
    return (na_k, na_v, swa_k, swa_v, ckv, kr)


def query_heads(u, q_norm_g, w_uq, w_uk, rope):
    B, n, _ = u.shape
    na_q = u[..., OFF_NA_Q:OFF_SWA_Q].reshape(B, n, NA_HEADS, HEAD_DIM)
    swa_q = u[..., OFF_SWA_Q:OFF_MLA_CQ].reshape(B, n, SWA_HEADS, HEAD_DIM)
    cq = rms_norm(u[..., OFF_MLA_CQ:OFF_POOL], q_norm_g)
    q = (cq @ w_uq).reshape(B, n, MLA_HEADS, MLA_NOPE + MLA_ROPE)
    q_nope, q_rope = q[..., :MLA_NOPE], q[..., MLA_NOPE:]
    if rope is not None:
        cos_h, sin_h, cos_r, sin_r = rope
        swa_q = apply_rope(swa_q, cos_h[:, None], sin_h[:, None])
        q_rope = apply_rope(q_rope, cos_r[:, None], sin_r[:, None])
    q_lat = jnp.einsum('bnhd,chd->bnhc', q_nope, w_uk)
    pool_in = u[..., OFF_POOL:OFF_GATE]
    gate_logits = u[..., OFF_GATE:]
    return (na_q, swa_q, q_lat, q_rope, pool_in, gate_logits)


def multiscale_pool(u, w_grp, scale):
    B, n, _ = u.shape
    ug = u.reshape(B, n, POOL_GROUPS, POOL_GROUP_DIM).astype(F32)
    cs = jnp.concatenate([jnp.zeros_like(ug[:, :1]), jnp.cumsum(ug, axis=1)], axis=1)
    t = jnp.arange(n)
    means = []
    for g, w in enumerate(POOL_WINDOWS):
        lo = jnp.clip(t - w // 2, 0, n)
        hi = jnp.clip(t - w // 2 + w, 0, n)
        cnt = (hi - lo).astype(F32)[None, :, None]
        means.append((cs[:, hi, g] - cs[:, lo, g]) / cnt)
    y = (jnp.stack(means, axis=2) - ug).astype(u.dtype)
    y = jnp.einsum('bngc,gcd->bngd', y, w_grp).reshape(B, n, POOL_W)
    return y * scale


def neighbourhood_attention(q, k, v, kc, vc, rpb):
    B, n, H, dh = q.shape
    rows = n // GRID_W
    kr_ = min(NA_ROWS, rows)
    m = kr_ * NA_COLS
    qg = q.reshape(B, rows, GRID_W, H, dh)
    kg = k.reshape(B, rows, GRID_W, H, dh)
    vg = v.reshape(B, rows, GRID_W, H, dh)
    col = jnp.arange(GRID_W)
    col_idx = jnp.clip(col - NA_COLS // 2, 0, GRID_W - NA_COLS)[:, None] + jnp.arange(NA_COLS)[None, :]
    col_off = col_idx - col[:, None] + (NA_COLS - 1)
    scale = dh ** -0.5

    def one_row(r):
        r0 = jnp.clip(r - kr_ // 2, 0, rows - kr_)
        qr = lax.dynamic_index_in_dim(qg, r, axis=1, keepdims=False)
        kb = lax.dynamic_slice_in_dim(kg, r0, kr_, axis=1)[:, :, col_idx]
        vb = lax.dynamic_slice_in_dim(vg, r0, kr_, axis=1)[:, :, col_idx]
        row_off = r0 + jnp.arange(kr_) - r + (NA_ROWS - 1)
        bias = rpb[:, row_off[None, :, None], col_off[:, None, :]]
        s_loc = jnp.einsum('bqhd,bjqkhd->bhqjk', qr, kb).astype(F32) * scale + bias.astype(F32)
        s_ctx = jnp.einsum('bqhd,blhd->bhql', qr, kc).astype(F32) * scale
        p = jax.nn.softmax(jnp.concatenate([s_loc.reshape(B, H, GRID_W, m), s_ctx], axis=-1), axis=-1)
        p = p.astype(v.dtype)
        return (jnp.einsum('bhqjk,bjqkhd->bqhd', p[..., :m].reshape(B, H, GRID_W, kr_, NA_COLS), vb)
                + jnp.einsum('bhql,blhd->bqhd', p[..., m:], vc))

    o = lax.map(one_row, jnp.arange(rows))
    return jnp.moveaxis(o, 0, 1).reshape(B, n, H * dh)


def banded(t, nb):
    B, _, KVH, dh = t.shape
    tb = t.reshape(B, nb, SWA_BLOCK, KVH, dh)
    z = jnp.zeros_like(tb[:, :1])
    prev = jnp.concatenate([z, tb[:, :-1]], axis=1)
    nxt = jnp.concatenate([tb[:, 1:], z], axis=1)
    return jnp.concatenate([prev, tb, nxt], axis=2)


def windowed_attention(q, k, v, kc, vc, sink):
    B, n, H, dh = q.shape
    KVH = k.shape[2]
    G = H // KVH
    L = kc.shape[1]
    nb = n // SWA_BLOCK
    m = 3 * SWA_BLOCK
    scale = dh ** -0.5
    qb = q.reshape(B, nb, SWA_BLOCK, KVH, G, dh)
    kb = banded(k, nb)
    vb = banded(v, nb)
    qpos = jnp.arange(SWA_BLOCK)
    kpos = jnp.arange(m) - SWA_BLOCK
    kabs = jnp.arange(nb)[:, None] * SWA_BLOCK + kpos[None, :]
    rel = kpos[None, :] - qpos[:, None]
    mask = (jnp.abs(rel) <= SWA_WINDOW)[None] & ((kabs >= 0) & (kabs < n))[:, None, :]
    s_loc = jnp.einsum('bnqkgd,bnjkd->bnkgqj', qb, kb).astype(F32) * scale
    s_loc = jnp.where(mask[None, :, None, None], s_loc, NEG_INF)
    s_ctx = jnp.einsum('bnqkgd,blkd->bnkgql', qb, kc).astype(F32) * scale
    s_sink = jnp.broadcast_to(sink.astype(F32).reshape(1, 1, KVH, G, 1, 1), s_ctx.shape[:-1] + (1,))
    p = jax.nn.softmax(jnp.concatenate([s_loc, s_ctx, s_sink], axis=-1), axis=-1).astype(v.dtype)
    o = (jnp.einsum('bnkgqj,bnjkd->bnqkgd', p[..., :m], vb)
         + jnp.einsum('bnkgql,blkd->bnqkgd', p[..., m:m + L], vc))
    return o.reshape(B, n, H * dh)


def dense_attention(q, k, v, sink):
    B, n, H, dh = q.shape
    KVH = k.shape[2]
    G = H // KVH
    qg = q.reshape(B, n, KVH, G, dh)
    s = jnp.einsum('bqkgd,bjkd->bkgqj', qg, k).astype(F32) * (dh ** -0.5)
    m = s.shape[-1]
    if sink is not None:
        s = jnp.concatenate([s, jnp.broadcast_to(sink.astype(F32).reshape(1, KVH, G, 1, 1), s.shape[:-1] + (1,))], axis=-1)
    p = jax.nn.softmax(s, axis=-1).astype(v.dtype)[..., :m]
    return jnp.einsum('bkgqj,bjkd->bqkgd', p, v).reshape(B, n, H * dh)


def mla_attend(q_lat, q_rope, ckv, kr, w_uv):
    B, n, H, _ = q_lat.shape
    s = (jnp.einsum('bqhc,bjc->bhqj', q_lat, ckv) + jnp.einsum('bqhr,bjr->bhqj', q_rope, kr)).astype(F32) * MLA_SCALE
    p = jax.nn.softmax(s, axis=-1).astype(ckv.dtype)
    o_lat = jnp.einsum('bhqj,bjc->bqhc', p, ckv)
    return jnp.einsum('bqhc,chd->bqhd', o_lat, w_uv).reshape(B, n, H * MLA_V)


def mla_blocks(q_lat, q_rope, ckv, kr, w_uv):
    B, n, H, C = q_lat.shape
    nb = n // MLA_Q_BLOCK
    qb = jnp.moveaxis(q_lat.reshape(B, nb, MLA_Q_BLOCK, H, C), 1, 0)
    rb = jnp.moveaxis(q_rope.reshape(B, nb, MLA_Q_BLOCK, H, MLA_ROPE), 1, 0)
    o = lax.map(lambda a: mla_attend(a[0], a[1], ckv, kr, w_uv), (qb, rb))
    return jnp.moveaxis(o, 0, 1).reshape(B, n, H * MLA_V)


def latent_branches(q, kv, kv_c, pool_w, pool_scale, rpb, sink, w_uv):
    na_q, swa_q, q_lat, q_rope, pool_in, _ = q
    na_k, na_v, swa_k, swa_v, ckv, kr = kv
    na_kc, na_vc, swa_kc, swa_vc, ckv_c, kr_c = kv_c
    return (multiscale_pool(pool_in, pool_w, pool_scale),
            neighbourhood_attention(na_q, na_k, na_v, na_kc, na_vc, rpb),
            windowed_attention(swa_q, swa_k, swa_v, swa_kc, swa_vc, sink),
            mla_blocks(q_lat, q_rope, jnp.concatenate([ckv_c, ckv], axis=1),
                       jnp.concatenate([kr_c, kr], axis=1), w_uv))


def context_branches(q, kv_c, pool_w, pool_scale, sink, w_uv):
    na_q, swa_q, q_lat, q_rope, pool_in, _ = q
    na_kc, na_vc, swa_kc, swa_vc, ckv_c, kr_c = kv_c
    return (multiscale_pool(pool_in, pool_w, pool_scale),
            dense_attention(na_q, na_kc, na_vc, None),
            dense_attention(swa_q, swa_kc, swa_vc, sink),
            mla_attend(q_lat, q_rope, ckv_c, kr_c, w_uv))


def merge_branches(branches, gate_logits, w_branch, w_out):
    merged = None
    for i, y in enumerate(branches):
        term = jax.nn.sigmoid(gate_logits[..., i * D_MODEL:(i + 1) * D_MODEL]) * (y @ w_branch[i])
        merged = term if merged is None else merged + term
    return merged @ w_out


def swiglu(h, w1, w3, w2):
    return (jax.nn.silu(h @ w1) * (h @ w3)) @ w2


def setup_inputs(seed: int = 0) -> dict:
    key = jax.random.key(seed)
    ks = jax.random.split(key, 24)
    D = D_MODEL
    L = DEPTH

    def nrm(k, shape, s):
        return jax.random.normal(k, shape, F32) * s

    return {
        "x": nrm(ks[0], (BATCH, SEQ, D), 1.0),
        "c": nrm(ks[1], (BATCH, D), 1.0),
        "ctx": nrm(ks[2], (BATCH, CTX_LEN, D), 1.0),
        "c_ctx": nrm(ks[3], (D,), 1.0),
        "ada_w": nrm(ks[4], (L, D, 6 * D), D ** -0.5),
        "ada_b": nrm(ks[5], (L, 6 * D), 0.01),
        "norm1_g": 1.0 + nrm(ks[6], (L, D), 0.1),
        "norm2_g": 1.0 + nrm(ks[7], (L, D), 0.1),
        "w_in": nrm(ks[8], (L, D, IN_COLS), D ** -0.5),
        "pool_w": nrm(ks[9], (L, POOL_GROUPS, POOL_GROUP_DIM, POOL_GROUP_DIM), POOL_GROUP_DIM ** -0.5),
        "pool_scale": 1.0 + nrm(ks[10], (L, POOL_W), 0.1),
        "na_rpb": nrm(ks[11], (L, NA_HEADS, 2 * NA_ROWS - 1, 2 * NA_COLS - 1), 0.1),
        "swa_sink": nrm(ks[12], (L, SWA_HEADS), 0.5),
        "mla_q_norm": 1.0 + nrm(ks[13], (L, MLA_Q_RANK), 0.1),
        "mla_kv_norm": 1.0 + nrm(ks[14], (L, MLA_KV_RANK), 0.1),
        "mla_w_uq": nrm(ks[15], (L, MLA_Q_RANK, MLA_HEADS * (MLA_NOPE + MLA_ROPE)), MLA_Q_RANK ** -0.5),
        "mla_w_uk": nrm(ks[16], (L, MLA_KV_RANK, MLA_HEADS, MLA_NOPE), MLA_KV_RANK ** -0.5),
        "mla_w_uv": nrm(ks[17], (L, MLA_KV_RANK, MLA_HEADS, MLA_V), MLA_KV_RANK ** -0.5),
        "w_branch": nrm(ks[18], (L, N_BRANCH, BRANCH_W, D), BRANCH_W ** -0.5),
        "w_out": nrm(ks[19], (L, D, D), D ** -0.5),
        "ffn_w1": nrm(ks[20], (L, D, FFN_HIDDEN), D ** -0.5),
        "ffn_w3": nrm(ks[21], (L, D, FFN_HIDDEN), D ** -0.5),
        "ffn_w2": nrm(ks[22], (L, FFN_HIDDEN, D), FFN_HIDDEN ** -0.5),
        "final_norm_g": 1.0 + nrm(ks[23], (D,), 0.1),
    }


def reference(x, c, ctx, c_ctx, ada_w, ada_b, norm1_g, norm2_g, w_in, pool_w, pool_scale,
              na_rpb, swa_sink, mla_q_norm, mla_kv_norm, mla_w_uq, mla_w_uk, mla_w_uv,
              w_branch, w_out, ffn_w1, ffn_w3, ffn_w2, final_norm_g):
    n = x.shape[1]
    cos_h, sin_h = axial_rope(n, HEAD_DIM)
    cos_r, sin_r = axial_rope(n, MLA_ROPE)
    rope = (cos_h, sin_h, cos_r, sin_r)
    xc = ctx
    for l in range(DEPTH):
        last = l == DEPTH - 1
        sh1, sc1, g1, sh2, sc2, g2 = modulation(c, ada_w[l], ada_b[l], 6)
        if last:
            csh1, csc1 = modulation(c_ctx, ada_w[l], ada_b[l], 2)
        else:
            csh1, csc1, cg1, csh2, csc2, cg2 = modulation(c_ctx, ada_w[l], ada_b[l], 6)
        hx = modulate(x, norm1_g[l], sh1, sc1)
        hc = modulate(xc, norm1_g[l], csh1, csc1)
        ux = hx @ w_in[l]
        uc = hc @ (w_in[l][:, :KV_COLS] if last else w_in[l])
        kv_c = kv_heads(uc[..., :KV_COLS], mla_kv_norm[l], None)
        kv_x = kv_heads(ux[..., :KV_COLS], mla_kv_norm[l], rope)
        q_x = query_heads(ux, mla_q_norm[l], mla_w_uq[l], mla_w_uk[l], rope)
        br_x = latent_branches(q_x, kv_x, kv_c, pool_w[l], pool_scale[l], na_rpb[l], swa_sink[l], mla_w_uv[l])
        mixed_x = merge_branches(br_x, q_x[5], w_branch[l], w_out[l])
        if not last:
            q_c = query_heads(uc, mla_q_norm[l], mla_w_uq[l], mla_w_uk[l], None)
            br_c = context_branches(q_c, kv_c, pool_w[l], pool_scale[l], swa_sink[l], mla_w_uv[l])
            xc = xc + cg1 * merge_branches(br_c, q_c[5], w_branch[l], w_out[l])
            xc = xc + cg2 * swiglu(modulate(xc, norm2_g[l], csh2, csc2), ffn_w1[l], ffn_w3[l], ffn_w2[l])
        x = x + g1[:, None, :] * mixed_x
        x = x + g2[:, None, :] * swiglu(modulate(x, norm2_g[l], sh2, sc2), ffn_w1[l], ffn_w3[l], ffn_w2[l])
    return rms_norm(x, final_norm_g)
```

```python
import numpy as np
from contextlib import ExitStack
import concourse.bass as bass
import concourse.mybir as mybir
from concourse.bass_utils import run_bass_kernel_spmd

F32 = mybir.dt.float32
BF16 = mybir.dt.bfloat16
AF = mybir.ActivationFunctionType
ALU = mybir.AluOpType

D = 1024
KC = 8
L = 2
CTX = 256
GRID_W = 64
FFN = 2816
FC = 22
NEG = -1e30
EPS = 1e-6
MLA_SCALE = 96 ** -0.5
NCORES = 8
CPB = 4


class Buf:
    __slots__ = ("name", "w", "rs")

    def __init__(self, name=""):
        self.name = name
        self.w = None
        self.rs = []


class Op:
    __slots__ = ("eng", "fn", "deps", "dma", "signal", "sem", "target", "prevwait")


class Prog:
    CE = ("pe", "act", "dve", "pool")
    DQ = ("sp", "pool", "act")

    def __init__(self, nc, stack, ndsem=8):
        self.nc = nc
        self.nds = ndsem
        self.csem = {e: stack.enter_context(nc.semaphore("cs_" + e)) for e in self.CE}
        self.ccnt = {e: 0 for e in self.CE}
        self.dsem = {q: [stack.enter_context(nc.semaphore("ds_%s%d" % (q, i))) for i in range(ndsem)] for q in self.DQ}
        self.dcnt = {q: 0 for q in self.DQ}
        self.ops = []
        self.n_inst = 0
        self.touched = set()

    def op(self, eng, fn, reads=(), writes=(), dma=False):
        o = Op()
        o.eng = eng
        o.fn = fn
        o.dma = dma
        o.signal = dma
        o.sem = None
        o.target = 0
        o.prevwait = None
        deps = set()
        for b in reads:
            if b.w is not None:
                deps.add(b.w)
        for b in writes:
            if b.w is not None:
                deps.add(b.w)
            for r in b.rs:
                deps.add(r)
        deps.discard(o)
        if eng == "pe" and not dma:
            deps = {d for d in deps if d.dma or d.eng != "pe"}
        o.deps = deps
        for b in reads:
            b.rs.append(o)
            self.touched.add(b)
        for b in writes:
            b.w = o
            b.rs = []
            self.touched.add(b)
        self.ops.append(o)
        return o

    def dma(self, q, out, in_, reads=(), writes=(), **kw):
        return self.op(q, lambda e: e.dma_start(out=out, in_=in_, **kw), reads, writes, dma=True)

    def flush(self):
        ops = self.ops
        self.ops = []
        for o in ops:
            for d in o.deps:
                d.signal = True
        for o in ops:
            if o.dma:
                q = o.eng
                i = self.dcnt[q]
                self.dcnt[q] += 1
                o.sem = self.dsem[q][i % self.nds]
                o.target = 16 * (i // self.nds + 1)
                if i >= self.nds:
                    o.prevwait = (o.sem, 16 * (i // self.nds))
            elif o.signal:
                self.ccnt[o.eng] += 1
                o.sem = self.csem[o.eng]
                o.target = self.ccnt[o.eng]
        per = {e: [] for e in ("pe", "act", "dve", "pool", "sp")}
        for o in ops:
            per[o.eng].append(o)
        final_d = {q: [(self.dsem[q][s], 16 * ((self.dcnt[q] - 1 - s) // self.nds + 1))
                       for s in range(self.nds) if self.dcnt[q] > s] for q in self.DQ}

        def emit(name, e):
            waited = {}
            for o in per[name]:
                need = {}
                for d in o.deps:
                    key = id(d.sem)
                    if key not in need or need[key][1] < d.target:
                        need[key] = (d.sem, d.target)
                if o.prevwait is not None:
                    key = id(o.prevwait[0])
                    if key not in need or need[key][1] < o.prevwait[1]:
                        need[key] = o.prevwait
                for key, (sem, tgt) in need.items():
                    if waited.get(key, 0) >= tgt:
                        continue
                    e.wait_ge(sem, tgt)
                    waited[key] = tgt
                    self.n_inst += 1
                inst = o.fn(e)
                self.n_inst += 1
                if o.signal:
                    inst.then_inc(o.sem, 16 if o.dma else 1)
            if name in final_d:
                for sem, tgt in final_d[name]:
                    if waited.get(id(sem), 0) < tgt:
                        e.wait_ge(sem, tgt)

        with self.nc.Block() as block:
            @block.tensor
            def _(e):
                emit("pe", e)

            @block.scalar
            def _(e):
                emit("act", e)

            @block.vector
            def _(e):
                emit("dve", e)

            @block.gpsimd
            def _(e):
                emit("pool", e)

            @block.sync
            def _(e):
                emit("sp", e)
        for b in self.touched:
            b.w = None
            b.rs = []
        self.touched = set()


class Ctx:
    pass


def bufs(n, name=""):
    return [Buf("%s%d" % (name, i)) for i in range(n)]


OFF_NA_K, OFF_NA_V, OFF_SWA_K, OFF_SWA_V, OFF_CKV, OFF_KR = 0, 256, 512, 640, 768, 896
OFF_NA_Q, OFF_SWA_Q, OFF_CQ, OFF_POOL, OFF_GATE = 928, 1184, 1440, 1696, 1952
IN_COLS = 6048


def _swap_idx(base, nheads, hd):
    idx = []
    for h in range(nheads):
        b = base + h * hd
        idx += list(range(b + hd // 2, b + hd)) + list(range(b, b + hd // 2))
    return idx


W1_IDX = (list(range(OFF_NA_K, OFF_NA_K + 256)) + list(range(OFF_SWA_K, OFF_SWA_K + 128)) + _swap_idx(OFF_SWA_K, 2, 64)
          + list(range(OFF_CKV, OFF_CKV + 128)) + list(range(OFF_KR, OFF_KR + 32)) + _swap_idx(OFF_KR, 1, 32)
          + list(range(OFF_NA_Q, OFF_NA_Q + 256)) + list(range(OFF_SWA_Q, OFF_SWA_Q + 256)) + _swap_idx(OFF_SWA_Q, 4, 64)
          + list(range(OFF_CQ, OFF_CQ + 256)) + list(range(OFF_POOL, OFF_POOL + 256))
          + list(range(OFF_NA_V, OFF_NA_V + 256)) + list(range(OFF_SWA_V, OFF_SWA_V + 128)) + list(range(OFF_CKV, OFF_CKV + 128)))
W1C = len(W1_IDX)
C_NAK, C_SWAK, C_SWAK_SW, C_CKV, C_KR, C_KR_SW = 0, 256, 384, 512, 640, 672
C_NAQ, C_SWAQ, C_SWAQ_SW, C_CQ, C_POOL, C_V = 704, 960, 1216, 1472, 1728, 1984
WQR_IDX = [h * 96 + 64 + r for h in range(4) for r in range(32)]
WQR_SW_IDX = [h * 96 + 64 + ((r + 16) % 32) for h in range(4) for r in range(32)]

R_NAK, R_SWAK, R_CKV, R_POOL, R_KR, EXK_ROWS = 0, 256, 384, 512, 768, 800


class Cfg:
    def __init__(self, NT):
        self.NT = NT
        self.TC = 512 * NT
        self.S = CPB * self.TC
        self.ROWS = self.S // GRID_W
        self.NKC = (CTX + self.S) // 128


def dap(t, pat, **kw):
    return t.rearrange(pat, **kw)


class Builder:
    def __init__(self, cfg, phases, ext_in, ext_out):
        self.cfg = cfg
        self.nc = bass.Bass("TRN2", target_bir_lowering=False)
        self.ext_in = set(ext_in)
        self.ext_out = set(ext_out)
        self.drams = {}
        self.used_in = []
        self.used_out = []

    def dram(self, name, shape, dt):
        if name in self.drams:
            return self.drams[name]
        if name in self.ext_in:
            kind = "ExternalInput"
            self.used_in.append(name)
        elif name in self.ext_out:
            kind = "ExternalOutput"
            self.used_out.append(name)
        else:
            kind = "Internal"
        ap = self.nc.dram_tensor(name, list(shape), dt, kind=kind).ap()
        self.drams[name] = ap
        return ap

    def alloc(self, st, name, shape, dt):
        self.uid = getattr(self, "uid", 0) + 1
        return st.enter_context(self.nc.sbuf_tensor("sb%d_%s" % (self.uid, name), list(shape), dt))

    def bank(self):
        i = self.rot
        self.rot = (self.rot + 1) % 8
        return i

    def setup(self, st):
        nc = self.nc
        self.P = Prog(nc, st)
        P = self.P
        self.ps = [st.enter_context(nc.psum_tensor("psb%d" % i, [128, 512], F32)) for i in range(8)]
        self.bps = bufs(8, "ps")
        self.rot = 0
        self.ones32 = self.alloc(st, "ones32", [128, 128], F32)
        self.onesb = self.alloc(st, "onesb", [128, 128], BF16)
        self.identb = self.alloc(st, "identb", [128, 128], BF16)
        self.epsb = self.alloc(st, "epsb", [128, 1], F32)
        self.bconst = Buf("const")
        P.op("dve", lambda e: e.memset(self.ones32[:], 1.0), writes=[self.bconst])
        P.op("dve", lambda e: e.memset(self.onesb[:], 1.0), writes=[self.bconst])
        P.op("dve", lambda e: e.memset(self.epsb[:], EPS), writes=[self.bconst])
        ident = self.dram("ident", [128, 128], F32)
        P.dma("pool", self.identb[:], ident, writes=[self.bconst])
        self.mod = [self.alloc(st, "mod%d" % l, [128, 48, 2], F32) for l in range(L)]
        self.gsc1 = [self.alloc(st, "gsc1_%d" % l, [128, 8, 2], F32) for l in range(L)]
        self.gsc2 = [self.alloc(st, "gsc2_%d" % l, [128, 8, 2], F32) for l in range(L)]
        self.bmod = [Buf("mod%d" % l) for l in range(L)]
        self.smallv = self.alloc(st, "smallv", [128, L, 32], F32)
        self.bsmall = Buf("smallv")
        sv = self.dram("smallv", [128, L, 32], F32)
        P.dma("sp", self.smallv[:], sv, writes=[self.bsmall])
        P.op("act", lambda e: e.activation(self.smallv[:, :, 25:29], self.smallv[:, :, 21:25], AF.Exp), reads=[self.bsmall], writes=[self.bsmall])
        self.kvgrow = self.alloc(st, "kvgrow", [128, L, 128], F32)
        P.dma("sp", self.kvgrow[:], self.dram("kvg_row", [128, L, 128], F32), writes=[self.bsmall])
        self.adab = self.alloc(st, "adab", [128, L, 48], F32)
        P.dma("sp", self.adab[:], self.dram("ada_bT", [128, L, 48], F32), writes=[self.bsmall])
        self.cond = self.alloc(st, "cond", [128, 8, 2], F32)
        self.scond = self.alloc(st, "scond", [128, 8, 2], F32)
        self.bcond = Buf("cond")
        P.dma("sp", self.cond[:], self.dram("cond", [128, 8, 2], F32), writes=[self.bcond])
        P.op("act", lambda e: e.activation(self.scond[:], self.cond[:], AF.Silu), reads=[self.bcond], writes=[self.bcond])
        P.flush()

    def phase_mod(self, l):
        P = self.P
        nc = self.nc
        ada_w = self.dram("ada_w%d" % l, [D, 6 * D], F32)
        with ExitStack() as st:
            wblk = self.alloc(st, "wblk", [128, 2, KC, 512], F32)
            tmp = self.alloc(st, "modtmp", [128, 8, 2], F32)
            bw = bufs(2, "wblk")
            btmp = Buf("modtmp")
            for jb in range(12):
                s = jb % 2
                P.dma("sp", wblk[:, s], ada_w[:, jb * 512:(jb + 1) * 512].rearrange("(k p) n -> p k n", p=128), writes=[bw[s]])
                for j in range(4):
                    b = self.bank()
                    jj = jb * 4 + j
                    for k in range(KC):
                        P.op("pe", lambda e, b=b, s=s, k=k, j=j: e.matmul(self.ps[b][:, 0:2], wblk[:, s, k, j * 128:(j + 1) * 128], self.scond[:, k, :], start=(k == 0), stop=(k == KC - 1)),
                             reads=[bw[s], self.bcond], writes=[self.bps[b]])
                    P.op("act", lambda e, b=b, jj=jj: e.activation(self.mod[l][:, jj, :], self.ps[b][:, 0:2], AF.Identity, bias=self.adab[:, l, jj:jj + 1], scale=1.0),
                         reads=[self.bps[b], self.bsmall], writes=[self.bmod[l]])
            for (gsc, off, goff) in ((self.gsc1[l], 8, 0), (self.gsc2[l], 32, 8)):
                P.op("dve", lambda e, off=off: e.tensor_scalar(tmp[:], self.mod[l][:, off:off + 8, :], 1.0, None, ALU.add), reads=[self.bmod[l]], writes=[btmp])
                for c in range(2):
                    P.op("dve", lambda e, gsc=gsc, c=c, goff=goff: e.tensor_tensor(gsc[:, :, c], tmp[:, :, c], self.smallv[:, l, goff:goff + 8], ALU.mult),
                         reads=[btmp, self.bsmall], writes=[self.bmod[l]])
            P.flush()

    def norm_mod(self, xt, bx, h, bh, sq, bsq, r, br, gsc, shift, cidx, N, l, bxl=None):
        P = self.P
        if bxl is None:
            bxl = [bx] * KC
        P.op("act", lambda e: e.activation(sq[:, :, :N], xt[:, :, :N], AF.Square), reads=list(set(bxl)), writes=bsq)
        b = self.bank()
        for k in range(KC):
            P.op("pe", lambda e, k=k: e.matmul(self.ps[b][:, :N], self.ones32[:], sq[:, k, :N], start=(k == 0), stop=(k == KC - 1)),
                 reads=[self.bconst, bsq[k]], writes=[self.bps[b]])
        P.op("act", lambda e: e.activation(r[:, :N], self.ps[b][:, :N], AF.Sqrt, bias=self.epsb[:], scale=1.0 / D), reads=[self.bps[b], self.bconst], writes=[br])
        P.op("dve", lambda e: e.reciprocal(r[:, :N], r[:, :N]), reads=[br], writes=[br])
        for k in range(KC):
            P.op("dve", lambda e, k=k: e.scalar_tensor_tensor(sq[:, k, :N], xt[:, k, :N], gsc[:, k, cidx:cidx + 1], r[:, :N], ALU.mult, ALU.mult),
                 reads=[bxl[k], self.bmod[l], br], writes=[bsq[k]])
            P.op("act", lambda e, k=k: e.activation(h[:, k, :N], sq[:, k, :N], AF.Identity, bias=shift[:, k, cidx:cidx + 1], scale=1.0),
                 reads=[bsq[k], self.bmod[l]], writes=[bh])

    def cast_load(self, dst, src, ncols, writes, q="pool"):
        for c0 in range(0, ncols, 1024):
            c1 = min(ncols, c0 + 1024)
            self.P.dma(q, dst[:, :, c0:c1], src[:, c0:c1].rearrange("(k p) n -> p k n", p=128), writes=writes)

    def proj(self, w, bw, c0, m, h, bh, N, kc=KC):
        b = self.bank()
        for k in range(kc):
            self.P.op("pe", lambda e, k=k: e.matmul(self.ps[b][:m, :N], w[:, k, c0:c0 + m], h[:, k, :N], start=(k == 0), stop=(k == kc - 1)),
                      reads=[bw, bh], writes=[self.bps[b]])
        return b

    def phase1(self, l, xname):
        P = self.P
        cfg = self.cfg
        TC, NT = cfg.TC, cfg.NT
        xT = self.dram(xname, [D, TC], F32)
        ctxT = self.dram("ctxT" if l == 0 else "xcT%d" % l, [D, CTX], F32)
        w1d = self.dram("w1_%d" % l, [D, W1C], F32)
        wuqd = self.dram("w_uq%d" % l, [256, 384], F32)
        wqrd = self.dram("wqr%d" % l, [256, 256], F32)
        wukd = self.dram("w_ukT%d" % l, [4, 64, 128], F32)
        ropes = [self.dram(n, [128, TC], F32) for n in ("ropeh_c", "ropeh_s", "roper_c", "roper_s")]
        out = {}
        for pre, n in (("", TC), ("c_", CTX)):
            out[pre + "exk"] = self.dram("%sexk%d" % (pre, l), [EXK_ROWS, n], BF16)
            out[pre + "exv"] = self.dram("%sexv%d" % (pre, l), [n, 384], BF16)
            out[pre + "exvc"] = self.dram("%sexvc%d" % (pre, l), [n, 128], BF16)
            out[pre + "qs"] = self.dram("%sqs%d" % (pre, l), [512, n], BF16)
            out[pre + "qlat"] = self.dram("%sqlat%d" % (pre, l), [512, n], BF16)
            out[pre + "qrope"] = self.dram("%sqrope%d" % (pre, l), [128, n], BF16)
            out[pre + "hx"] = self.dram("%shx%d" % (pre, l), [D, n], BF16)
        with ExitStack() as st:
            A = lambda name, shape, dt: self.alloc(st, name, shape, dt)
            w1 = A("w1", [128, KC, W1C], BF16)
            wuq = A("wuq", [128, 2, 384], BF16)
            wqr = A("wqr", [128, 2, 256], BF16)
            wuk = A("wuk", [64, 4, 128], BF16)
            bw = Buf("w1")
            self.cast_load(w1, w1d, W1C, [bw])
            self.cast_load(wuq, wuqd, 384, [bw])
            self.cast_load(wqr, wqrd, 256, [bw])
            P.dma("pool", wuk[:], wukd.rearrange("h d c -> d h c"), writes=[bw])
            xt = [A("xt%d" % i, [128, KC, 512], F32) for i in range(2)]
            bx = bufs(2, "xt")
            sq = A("sq", [128, KC, 512], F32)
            bsq = bufs(KC, "sq")
            hh = [A("h%d" % i, [128, KC, 512], BF16) for i in range(2)]
            bh = bufs(2, "h")
            r = A("r", [128, 512], F32)
            br = Buf("r")
            rp = [A("rope%d" % i, [128, 4, 512], F32) for i in range(2)]
            brp = bufs(2, "rope")
            stg = [A("stg%d" % i, [128, 7, 512], BF16) for i in range(2)]
            bstg = [bufs(7, "stg%d_" % i) for i in range(2)]
            qstg = [A("qstg%d" % i, [128, 4, 512], BF16) for i in range(2)]
            bqstg = [bufs(4, "qstg%d_" % i) for i in range(2)]
            qlstg = [A("qlstg%d" % i, [128, 5, 512], BF16) for i in range(2)]
            bqlstg = [bufs(5, "qlstg%d_" % i) for i in range(2)]
            t1 = A("t1", [128, 2, 512], F32)
            bt = bufs(2, "t")
            cqsq = A("cqsq", [128, 2, 512], F32)
            bcqsq = bufs(2, "cqsq")
            r2 = A("r2", [128, 512], F32)
            br2 = Buf("r2")
            cqn = A("cqn", [128, 2, 512], BF16)
            bcqn = Buf("cqn")
            qn = A("qn", [64, 4, 512], BF16)
            bqn = bufs(4, "qn")
            vstg = [A("vstg%d" % i, [128, 4, 384], BF16) for i in range(2)]
            bvstg = bufs(2, "vstg")
            vcstg = A("vcstg", [128, 4, 128], BF16)
            bvc = Buf("vcstg")
            junk = A("junk", [128, 128], F32)
            ss = A("ss", [128, 4], F32)
            bss = Buf("ss")
            tiles = [("c_", CTX, ctxT, 0, 1)] + [("", 512, xT, t * 512, 0) for t in range(NT)]
            def do_tile(ti, pre, N, src, tok0, cidx):
                s = ti % 2
                rope = pre == ""
                NS = N // 128
                P.dma("sp", xt[s][:, :, :N], src[:, tok0:tok0 + N].rearrange("(k p) n -> p k n", p=128), writes=[bx[s]])
                if rope:
                    for i in range(4):
                        P.dma("sp", rp[s][:, i, :], ropes[i][:, tok0:tok0 + 512], writes=[brp[s]])
                h = hh[s]
                self.norm_mod(xt[s], bx[s], h, bh[s], sq, bsq, r, br, self.gsc1[l], self.mod[l][:, 0:8, :], cidx, N, l)
                P.dma("sp", out[pre + "hx"][:, tok0:tok0 + N].rearrange("(k p) n -> p k n", p=128), h[:, :, :N], reads=[bh[s]])
                S_, Q_, QL_ = stg[s], qstg[s], qlstg[s]
                bS, bQ, bQL = bstg[s], bqstg[s], bqlstg[s]

                def evac(b, m, dst, bdst, scale=1.0):
                    P.op("act", lambda e: e.activation(dst, self.ps[b][:m, :N], AF.Copy, scale=scale), reads=[self.bps[b]], writes=[bdst])

                def rope_pair(c_a, c_b, m, ci, si, scale, dst, bdst):
                    ba = self.proj(w1, bw, c_a, m, h, bh[s], N)
                    if not rope:
                        evac(ba, m, dst, bdst, scale)
                        return
                    bb = self.proj(w1, bw, c_b, m, h, bh[s], N)
                    P.op("dve", lambda e: e.scalar_tensor_tensor(t1[:m, 0, :N], self.ps[ba][:m, :N], scale, rp[s][:m, ci, :N], ALU.mult, ALU.mult),
                         reads=[self.bps[ba], brp[s]], writes=[bt[0]])
                    P.op("dve", lambda e: e.scalar_tensor_tensor(t1[:m, 1, :N], self.ps[bb][:m, :N], scale, rp[s][:m, si, :N], ALU.mult, ALU.mult),
                         reads=[self.bps[bb], brp[s]], writes=[bt[1]])
                    P.op("pool", lambda e: e.tensor_tensor(dst, t1[:m, 0, :N], t1[:m, 1, :N], ALU.add), reads=[bt[0], bt[1]], writes=[bdst])

                for j in range(2):
                    b = self.proj(w1, bw, C_NAK + j * 128, 128, h, bh[s], N)
                    evac(b, 128, S_[:, j, :N], bS[j])
                rope_pair(C_SWAK, C_SWAK_SW, 128, 0, 1, 1.0, S_[:, 2, :N], bS[2])
                b = self.proj(w1, bw, C_CKV, 128, h, bh[s], N)
                P.op("act", lambda e, b=b: e.activation(cqsq[:, 0, :N], self.ps[b][:, :N], AF.Square), reads=[self.bps[b]], writes=[bcqsq[0]])
                b2 = self.bank()
                P.op("pe", lambda e, b2=b2: e.matmul(self.ps[b2][:, :N], self.ones32[:], cqsq[:, 0, :N], start=True, stop=True),
                     reads=[self.bconst, bcqsq[0]], writes=[self.bps[b2]])
                P.op("act", lambda e, b2=b2: e.activation(r2[:, :N], self.ps[b2][:, :N], AF.Sqrt, bias=self.epsb[:], scale=1.0 / 128), reads=[self.bps[b2], self.bconst], writes=[br2])
                P.op("dve", lambda e: e.reciprocal(r2[:, :N], r2[:, :N]), reads=[br2], writes=[br2])
                P.op("dve", lambda e, b=b: e.scalar_tensor_tensor(S_[:, 3, :N], self.ps[b][:, :N], self.smallv[:, l, 20:21], r2[:, :N], ALU.mult, ALU.mult),
                     reads=[self.bps[b], self.bsmall, br2], writes=[bS[3]])
                rope_pair(C_KR, C_KR_SW, 32, 2, 3, 1.0, S_[:32, 4, :N], bS[4])
                for j in range(2):
                    b = self.proj(w1, bw, C_POOL + j * 128, 128, h, bh[s], N)
                    evac(b, 128, S_[:, 5 + j, :N], bS[5 + j])
                ek = out[pre + "exk"]
                P.dma("sp", ek[0:512, tok0:tok0 + N].rearrange("(c p) n -> p c n", p=128), S_[:, 0:4, :N], reads=bS[0:4])
                P.dma("sp", ek[R_POOL:R_POOL + 256, tok0:tok0 + N].rearrange("(c p) n -> p c n", p=128), S_[:, 5:7, :N], reads=bS[5:7])
                P.dma("sp", ek[R_KR:R_KR + 32, tok0:tok0 + N], S_[:32, 4, :N], reads=[bS[4]])
                for j in range(2):
                    b = self.proj(w1, bw, C_NAQ + j * 128, 128, h, bh[s], N)
                    evac(b, 128, Q_[:, j, :N], bQ[j], 0.125)
                for j in range(2):
                    rope_pair(C_SWAQ + j * 128, C_SWAQ_SW + j * 128, 128, 0, 1, 0.125, Q_[:, 2 + j, :N], bQ[2 + j])
                P.dma("sp", out[pre + "qs"][:, tok0:tok0 + N].rearrange("(c p) n -> p c n", p=128), Q_[:, :, :N], reads=bQ)
                bc = [self.proj(w1, bw, C_CQ + j * 128, 128, h, bh[s], N) for j in range(2)]
                for j in range(2):
                    P.op("act", lambda e, j=j: e.activation(cqsq[:, j, :N], self.ps[bc[j]][:, :N], AF.Square), reads=[self.bps[bc[j]]], writes=[bcqsq[j]])
                b2 = self.bank()
                for j in range(2):
                    P.op("pe", lambda e, j=j, b2=b2: e.matmul(self.ps[b2][:, :N], self.ones32[:], cqsq[:, j, :N], start=(j == 0), stop=(j == 1)),
                         reads=[self.bconst, bcqsq[j]], writes=[self.bps[b2]])
                P.op("act", lambda e, b2=b2: e.activation(r2[:, :N], self.ps[b2][:, :N], AF.Sqrt, bias=self.epsb[:], scale=1.0 / 256), reads=[self.bps[b2], self.bconst], writes=[br2])
                P.op("dve", lambda e: e.reciprocal(r2[:, :N], r2[:, :N]), reads=[br2], writes=[br2])
                for j in range(2):
                    P.op("dve", lambda e, j=j: e.scalar_tensor_tensor(cqn[:, j, :N], self.ps[bc[j]][:, :N], self.smallv[:, l, 18 + j:19 + j], r2[:, :N], ALU.mult, ALU.mult),
                         reads=[self.bps[bc[j]], self.bsmall, br2], writes=[bcqn])
                for hd in range(4):
                    b = self.proj(wuq, bw, hd * 96, 64, cqn, bcqn, N, kc=2)
                    P.op("act", lambda e, b=b, hd=hd: e.activation(qn[:, hd, :N], self.ps[b][:64, :N], AF.Copy), reads=[self.bps[b]], writes=[bqn[hd]])
                    b = self.bank()
                    P.op("pe", lambda e, b=b, hd=hd: e.matmul(self.ps[b][:, :N], wuk[:, hd, :], qn[:, hd, :N], start=True, stop=True),
                         reads=[bw, bqn[hd]], writes=[self.bps[b]])
                    evac(b, 128, QL_[:, hd, :N], bQL[hd], MLA_SCALE)
                ba = self.proj(wqr, bw, 0, 128, cqn, bcqn, N, kc=2)
                if rope:
                    bb = self.proj(wqr, bw, 128, 128, cqn, bcqn, N, kc=2)
                    P.op("dve", lambda e, ba=ba: e.scalar_tensor_tensor(t1[:, 0, :N], self.ps[ba][:, :N], MLA_SCALE, rp[s][:, 2, :N], ALU.mult, ALU.mult),
                         reads=[self.bps[ba], brp[s]], writes=[bt[0]])
                    P.op("dve", lambda e, bb=bb: e.scalar_tensor_tensor(t1[:, 1, :N], self.ps[bb][:, :N], MLA_SCALE, rp[s][:, 3, :N], ALU.mult, ALU.mult),
                         reads=[self.bps[bb], brp[s]], writes=[bt[1]])
                    P.op("pool", lambda e: e.tensor_tensor(QL_[:, 4, :N], t1[:, 0, :N], t1[:, 1, :N], ALU.add), reads=[bt[0], bt[1]], writes=[bQL[4]])
                else:
                    evac(ba, 128, QL_[:, 4, :N], bQL[4], MLA_SCALE)
                P.dma("sp", out[pre + "qlat"][:, tok0:tok0 + N].rearrange("(c p) n -> p c n", p=128), QL_[:, 0:4, :N], reads=bQL[0:4])
                P.dma("sp", out[pre + "qrope"][:, tok0:tok0 + N], QL_[:, 4, :N], reads=[bQL[4]])
                for sub in range(NS):
                    b = self.bank()
                    for k in range(KC):
                        P.op("pe", lambda e, b=b, k=k, sub=sub: e.matmul(self.ps[b][:, :512], h[:, k, sub * 128:(sub + 1) * 128], w1[:, k, C_V:C_V + 512], start=(k == 0), stop=(k == KC - 1)),
                             reads=[bw, bh[s]], writes=[self.bps[b]])
                    P.op("act", lambda e, b=b, sub=sub: e.activation(vstg[s][:, sub, :], self.ps[b][:, :384], AF.Copy), reads=[self.bps[b]], writes=[bvstg[s]])
                    P.op("act", lambda e, b=b, sub=sub: e.activation(junk[:, :], self.ps[b][:, 384:512], AF.Square, accum_out=ss[:, sub:sub + 1]), reads=[self.bps[b]], writes=[bss])
                    P.op("act", lambda e, sub=sub: e.activation(ss[:, sub:sub + 1], ss[:, sub:sub + 1], AF.Sqrt, bias=self.epsb[:], scale=1.0 / 128), reads=[bss, self.bconst], writes=[bss])
                    P.op("dve", lambda e, sub=sub: e.reciprocal(ss[:, sub:sub + 1], ss[:, sub:sub + 1]), reads=[bss], writes=[bss])
                    P.op("dve", lambda e, b=b, sub=sub: e.scalar_tensor_tensor(vcstg[:, sub, :], self.ps[b][:, 384:512], ss[:, sub:sub + 1], self.kvgrow[:, l, :], ALU.mult, ALU.mult),
                         reads=[self.bps[b], bss, self.bsmall], writes=[bvc])
                P.dma("sp", out[pre + "exv"][tok0:tok0 + N, :].rearrange("(s p) c -> p s c", p=128), vstg[s][:, :NS, :], reads=[bvstg[s]])
                P.dma("sp", out[pre + "exvc"][tok0:tok0 + N, :].rearrange("(s p) c -> p s c", p=128), vcstg[:, :NS, :], reads=[bvc])

            for ti, tl in enumerate(tiles):
                do_tile(ti, *tl)
            P.flush()


def _rope_tables(pos, dim):
    n_freq = dim // 4
    inv = np.power(np.float32(10000.0), -np.arange(n_freq, dtype=np.float32) / np.float32(n_freq)).astype(np.float32)
    row = (pos // GRID_W).astype(np.float32)
    col = (pos % GRID_W).astype(np.float32)
    ang = np.concatenate([row[:, None] * inv, col[:, None] * inv], axis=-1).astype(np.float32)
    return np.cos(ang).astype(np.float32), np.sin(ang).astype(np.float32)


def host_shared(inp):
    g = lambda k: np.asarray(inp[k], dtype=np.float32)
    sh = {}
    sh["ident"] = np.eye(128, dtype=np.float32)
    smallv = np.zeros((128, L, 32), np.float32)
    for l in range(L):
        smallv[:, l, 0:8] = g("norm1_g")[l].reshape(8, 128).T
        smallv[:, l, 8:16] = g("norm2_g")[l].reshape(8, 128).T
        smallv[:, l, 16:18] = g("pool_scale")[l].reshape(2, 128).T
        smallv[:, l, 18:20] = g("mla_q_norm")[l].reshape(2, 128).T
        smallv[:, l, 20] = g("mla_kv_norm")[l]
        smallv[:, l, 21:25] = g("swa_sink")[l][None, :]
    sh["smallv"] = smallv
    sh["kvg_row"] = np.ascontiguousarray(np.broadcast_to(g("mla_kv_norm")[None, :, :], (128, L, 128)))
    sh["ada_bT"] = np.ascontiguousarray(g("ada_b").reshape(L, 48, 128).transpose(2, 0, 1))
    sh["ada_w"] = g("ada_w")
    w_in = g("w_in")
    sh["w1"] = np.ascontiguousarray(w_in[:, :, W1_IDX])
    sh["w_uq"] = g("mla_w_uq")
    sh["wqr"] = np.ascontiguousarray(np.concatenate([g("mla_w_uq")[:, :, WQR_IDX], g("mla_w_uq")[:, :, WQR_SW_IDX]], axis=2))
    sh["w_ukT"] = np.ascontiguousarray(g("mla_w_uk").transpose(0, 2, 3, 1))
    return sh


def host_core(inp, cfg, core):
    b, j = core // CPB, core % CPB
    TC = cfg.TC
    pc = {}
    x = np.asarray(inp["x"], dtype=np.float32)
    pc["xT"] = np.ascontiguousarray(x[b, j * TC:(j + 1) * TC, :].T)
    pc["ctxT"] = np.ascontiguousarray(np.asarray(inp["ctx"], dtype=np.float32)[b].T)
    cond = np.zeros((128, 8, 2), np.float32)
    cond[:, :, 0] = np.asarray(inp["c"], dtype=np.float32)[b].reshape(8, 128).T
    cond[:, :, 1] = np.asarray(inp["c_ctx"], dtype=np.float32).reshape(8, 128).T
    pc["cond"] = cond
    pos = np.arange(j * TC, (j + 1) * TC)
    ch, sh_ = _rope_tables(pos, 64)
    cr, sr = _rope_tables(pos, 32)
    p = np.arange(128)
    ih = (p % 64) % 32
    sgn_h = np.where((p % 64) < 32, -1.0, 1.0).astype(np.float32)
    pc["ropeh_c"] = np.ascontiguousarray(ch[:, ih].T)
    pc["ropeh_s"] = np.ascontiguousarray((sh_[:, ih] * sgn_h[None, :]).T)
    ir = (p % 32) % 16
    sgn_r = np.where((p % 32) < 16, -1.0, 1.0).astype(np.float32)
    pc["roper_c"] = np.ascontiguousarray(cr[:, ir].T)
    pc["roper_s"] = np.ascontiguousarray((sr[:, ir] * sgn_r[None, :]).T)
    return pc


S_BANKS = (0, 1, 2)
O_BANKS = (3, 4)
L_BANKS = (5, 6)
M_BANK = 7


def _attend(self, N, dv, chunks, pb, bpb, out_ap, bout, lt, blt, esink=None, hsel=0):
    P = self.P
    self.acnt += 1
    bO = O_BANKS[self.acnt % 2]
    bL = L_BANKS[self.acnt % 2]
    n = len(chunks)

    def score(c):
        kn, terms, _, _ = chunks[c]
        bS = S_BANKS[self.scnt % 3]
        self.scnt += 1
        for i, (lh, rh, rd) in enumerate(terms):
            P.op("pe", lambda e, lh=lh, rh=rh, i=i, bS=bS, kn=kn: e.matmul(self.ps[bS][:kn, :N], lh, rh, start=(i == 0), stop=(i == len(terms) - 1)),
                 reads=rd, writes=[self.bps[bS]])
        return bS

    nxt = score(0)
    for c in range(n):
        bS = nxt
        if c + 1 < n:
            nxt = score(c + 1)
        kn, _, v_ap, v_rd = chunks[c]
        sl = self.pcnt % len(pb)
        self.pcnt += 1
        P.op("act", lambda e, bS=bS, sl=sl, kn=kn: e.activation(pb[sl][:kn, :N], self.ps[bS][:kn, :N], AF.Exp), reads=[self.bps[bS]], writes=[bpb[sl]])
        P.op("pe", lambda e, sl=sl, kn=kn, v_ap=v_ap, c=c: e.matmul(self.ps[bO][:dv, :N], v_ap, pb[sl][:kn, :N], start=(c == 0), stop=(c == n - 1)),
             reads=[bpb[sl]] + list(v_rd), writes=[self.bps[bO]])
        P.op("pe", lambda e, sl=sl, kn=kn, c=c: e.matmul(self.ps[bL][:dv, :N], self.onesb[:kn, :dv], pb[sl][:kn, :N], start=(c == 0), stop=(c == n - 1)),
             reads=[bpb[sl], self.bconst], writes=[self.bps[bL]])
    if esink is not None:
        P.op("dve", lambda e: e.tensor_scalar(lt[:dv, :N], self.ps[bL][:dv, :N], esink[:dv, hsel:hsel + 1], None, ALU.add), reads=[self.bps[bL], self.bsmall], writes=[blt])
        P.op("dve", lambda e: e.reciprocal(lt[:dv, :N], lt[:dv, :N]), reads=[blt], writes=[blt])
    else:
        P.op("dve", lambda e: e.reciprocal(lt[:dv, :N], self.ps[bL][:dv, :N]), reads=[self.bps[bL]], writes=[blt])
    P.op("dve", lambda e: e.tensor_tensor(out_ap, self.ps[bO][:dv, :N], lt[:dv, :N], ALU.mult), reads=[self.bps[bO], blt], writes=[bout])


def _phase2_mla(self, l, with_ctx):
    P = self.P
    cfg = self.cfg
    TC, NT, NKC = cfg.TC, cfg.NT, cfg.NKC
    NK = NKC * 128
    mk = self.dram("mk%d" % l, [160, NK], BF16)
    mv = self.dram("mv%d" % l, [NK, 128], BF16)
    wuvd = self.dram("w_uv%d" % l, [128, 256], F32)
    self.acnt = self.scnt = self.pcnt = 0
    with ExitStack() as st:
        A = lambda name, shape, dt: self.alloc(st, name, shape, dt)
        kT = A("kT", [128, NK], BF16)
        krT = A("krT", [32, NK], BF16)
        vT = A("vT", [128, NKC, 128], BF16)
        wuv = A("wuv", [128, 1, 256], BF16)
        bk = Buf("mlak")
        nsp = 4
        for i in range(nsp):
            c0, c1 = (NK * i // nsp) // 128 * 128, (NK * (i + 1) // nsp) // 128 * 128
            P.dma("sp", kT[:, c0:c1], mk[0:128, c0:c1], writes=[bk])
            P.dma("sp", krT[:, c0:c1], mk[128:160, c0:c1], writes=[bk])
            P.dma("sp", vT[:, c0 // 128:c1 // 128, :], mv[c0:c1, :].rearrange("(c p) d -> p c d", p=128), writes=[bk])
        self.cast_load(wuv, wuvd, 256, [bk])
        ql = [A("ql%d" % i, [128, 4, 512], BF16) for i in range(2)]
        qr = [A("qr%d" % i, [32, 4, 512], BF16) for i in range(2)]
        bq = bufs(2, "mq")
        pb = [A("pb%d" % i, [128, 512], BF16) for i in range(3)]
        bpb = bufs(3, "pb")
        lt = A("lt", [128, 512], F32)
        blt = Buf("lt")
        olat = [A("olat%d" % i, [128, 512], BF16) for i in range(2)]
        bol = bufs(2, "olat")
        ystg = [A("ystg%d" % i, [64, 4, 512], BF16) for i in range(2)]
        bys = [bufs(4, "ystg%d_" % i) for i in range(2)]
        tiles = [("", 512, t * 512, NKC) for t in range(NT)]
        if with_ctx:
            tiles = [("c_", CTX, 0, CTX // 128)] + tiles

        def do_tile(ti, pre, N, tok0, nkc):
            s = ti % 2
            n_tot = TC if pre == "" else CTX
            qlat = self.dram("%sqlat%d" % (pre, l), [512, n_tot], BF16)
            qrope = self.dram("%sqrope%d" % (pre, l), [128, n_tot], BF16)
            yb = self.dram("%syb%d" % (pre, l), [1024, n_tot], BF16)
            P.dma("sp", ql[s][:, :, :N], qlat[:, tok0:tok0 + N].rearrange("(h p) n -> p h n", p=128), writes=[bq[s]])
            P.dma("sp", qr[s][:, :, :N], qrope[:, tok0:tok0 + N].rearrange("(h p) n -> p h n", p=32), writes=[bq[s]])
            for h in range(4):
                chunks = []
                for c in range(nkc):
                    terms = [(kT[:, c * 128:(c + 1) * 128], ql[s][:, h, :N], [bk, bq[s]]),
                             (krT[:, c * 128:(c + 1) * 128], qr[s][:, h, :N], [bk, bq[s]])]
                    chunks.append((128, terms, vT[:, c, :], [bk]))
                o = olat[h % 2]
                _attend(self, N, 128, chunks, pb, bpb, o[:, :N], bol[h % 2], lt, blt)
                P.op("pe", lambda e, h=h, o=o: e.matmul(self.ps[M_BANK][:64, :N], wuv[:, 0, h * 64:(h + 1) * 64], o[:, :N], start=True, stop=True),
                     reads=[bk, bol[h % 2]], writes=[self.bps[M_BANK]])
                P.op("act", lambda e, h=h: e.activation(ystg[s][:, h, :N], self.ps[M_BANK][:64, :N], AF.Copy), reads=[self.bps[M_BANK]], writes=[bys[s][h]])
            P.dma("sp", yb[768:1024, tok0:tok0 + N].rearrange("(h p) n -> p h n", p=64), ystg[s][:, :, :N], reads=bys[s])

        for ti, tl in enumerate(tiles):
            do_tile(ti, *tl)
        P.flush()


Builder.phase2_mla = _phase2_mla


NA_TABW = 22 * 64


def _phase2_local(self, l, with_ctx):
    P = self.P
    cfg = self.cfg
    TC, NT = cfg.TC, cfg.NT
    nak = self.dram("nak%d" % l, [256, TC + 512], BF16)
    nav = self.dram("nav%d" % l, [TC + 512, 256], BF16)
    swk = self.dram("swk%d" % l, [128, TC + 256], BF16)
    swv = self.dram("swv%d" % l, [TC + 256, 128], BF16)
    pin_d = self.dram("poolin%d" % l, [256, TC + 16], BF16)
    cexk = self.dram("c_exk%d" % l, [EXK_ROWS, CTX], BF16)
    cexv = self.dram("c_exv%d" % l, [CTX, 384], BF16)
    natab_d = self.dram("na_tab%d" % l, [128, 4 * NA_TABW], F32)
    narm_d = self.dram("na_rowmask", [3, 128, 8 * 512], F32)
    swtab_d = self.dram("sw_tab", [128, 9 * 128], F32)
    swedge_d = self.dram("sw_edge", [128, 2 * 512], F32)
    icnt_d = self.dram("pool_icnt", [128, 2 * TC], F32)
    icntc_d = self.dram("pool_icnt_c", [128, 2 * CTX], F32)
    pwbd_d = self.dram("pool_wbd%d" % l, [128, 256], F32)
    self.acnt = self.scnt = self.pcnt = 0
    with ExitStack() as st:
        A = lambda name, shape, dt: self.alloc(st, name, shape, dt)
        natab = A("natab", [128, 1, 4 * NA_TABW], BF16)
        narm = A("narm", [128, 3, 8 * 512], BF16)
        swtab = A("swtab", [128, 1, 9 * 128], BF16)
        swedge = A("swedge", [128, 1, 2 * 512], BF16)
        pwbd = A("pwbd", [128, 1, 256], BF16)
        bt = Buf("tabs")
        self.cast_load(natab, natab_d, 4 * NA_TABW, [bt])
        for v in range(3):
            self.cast_load(narm[:, v:v + 1, :], narm_d[v], 8 * 512, [bt])
        self.cast_load(swtab, swtab_d, 9 * 128, [bt])
        self.cast_load(swedge, swedge_d, 2 * 512, [bt])
        self.cast_load(pwbd, pwbd_d, 256, [bt])
        cnk = A("cnk", [64, 4, CTX], BF16)
        csk = A("csk", [64, 2, CTX], BF16)
        cv = A("cv", [128, 2, 384], BF16)
        bc = Buf("ctxkv")
        P.dma("sp", cnk[:], cexk[R_NAK:R_NAK + 256, :].rearrange("(h p) n -> p h n", p=64), writes=[bc])
        P.dma("sp", csk[:], cexk[R_SWAK:R_SWAK + 128, :].rearrange("(h p) n -> p h n", p=64), writes=[bc])
        P.dma("sp", cv[:], cexv.rearrange("(c p) d -> p c d", p=128), writes=[bc])
        nk = [A("nk%d" % i, [64, 4, 960], BF16) for i in range(2)]
        nv = [A("nv%d" % i, [128, 8, 256], BF16) for i in range(2)]
        sk = [A("sk%d" % i, [64, 2, 768], BF16) for i in range(2)]
        sv = [A("sv%d" % i, [128, 6, 128], BF16) for i in range(2)]
        qq = [A("qq%d" % i, [64, 8, 512], BF16) for i in range(2)]
        bld = bufs(2, "p2ld")
        pb = [A("pb%d" % i, [128, 512], BF16) for i in range(3)]
        bpb = bufs(3, "pb")
        lt = A("lt", [128, 512], F32)
        blt = Buf("lt")
        ystg = [A("ystg%d" % i, [64, 8, 512], BF16) for i in range(2)]
        bys = [bufs(8, "ystg%d_" % i) for i in range(2)]
        pin = [A("pin%d" % i, [128, 2, 528], BF16) for i in range(2)]
        icnt = [A("icnt%d" % i, [128, 2, 512], F32) for i in range(2)]
        bpin = bufs(2, "pin")
        X = A("pX", [128, 2, 528], F32)
        A2 = A("pA2", [128, 2, 528], F32)
        B4 = A("pB4", [128, 2, 528], F32)
        B8 = A("pB8", [128, 528], F32)
        SUM = A("pSUM", [128, 2, 512], F32)
        Yb = A("pYb", [128, 2, 512], BF16)
        bpl = bufs(6, "pool")
        pstg = [A("pstg%d" % i, [128, 2, 512], BF16) for i in range(2)]
        bpstg = bufs(2, "pstg")
        esink = self.smallv[:, l, 25:29]
        tiles = [("", 512, t) for t in range(NT)]
        if with_ctx:
            tiles = [("c_", CTX, -1)] + tiles

        def do_tile(ti, pre, N, t):
            s = ti % 2
            lat = pre == ""
            n_tot = TC if lat else CTX
            tok0 = t * 512 if lat else 0
            qs = self.dram("%sqs%d" % (pre, l), [512, n_tot], BF16)
            yb = self.dram("%syb%d" % (pre, l), [1024, n_tot], BF16)
            P.dma("sp", qq[s][:, :, :N], qs[:, tok0:tok0 + N].rearrange("(h p) n -> p h n", p=64), writes=[bld[s]])
            if lat:
                P.dma("sp", nk[s][:], nak[:, tok0:tok0 + 960].rearrange("(h p) n -> p h n", p=64), writes=[bld[s]])
                P.dma("sp", nv[s][:, 0:7, :], nav[tok0:tok0 + 896, :].rearrange("(c p) d -> p c d", p=128), writes=[bld[s]])
                P.dma("sp", nv[s][0:64, 7, :], nav[tok0 + 896:tok0 + 960, :], writes=[bld[s]])
                P.dma("sp", sk[s][:], swk[:, tok0:tok0 + 768].rearrange("(h p) n -> p h n", p=64), writes=[bld[s]])
                P.dma("sp", sv[s][:], swv[tok0:tok0 + 768, :].rearrange("(c p) d -> p c d", p=128), writes=[bld[s]])
                P.dma("sp", pin[s][:], pin_d[:, tok0:tok0 + 528].rearrange("(c p) n -> p c n", p=128), writes=[bpin[s]])
                P.dma("sp", icnt[s][:], icnt_d[:, :].rearrange("p (c n) -> p c n", c=2)[:, :, tok0:tok0 + 512], writes=[bpin[s]])
            else:
                P.op("pool", lambda e: e.memset(pin[s][:], 0.0), writes=[bpin[s]])
                P.dma("sp", pin[s][:, :, 8:8 + CTX], cexk[R_POOL:R_POOL + 256, :].rearrange("(c p) n -> p c n", p=128), writes=[bpin[s]])
                P.dma("sp", icnt[s][:, :, :CTX], icntc_d[:, :].rearrange("p (c n) -> p c n", c=2), writes=[bpin[s]])
            v = 1 if 0 < t < NT - 1 else (0 if t == 0 else 2)
            for h in range(4):
                chunks = []
                if lat:
                    for c in range(8):
                        kn = 128 if c < 7 else 64
                        c0 = (14 - 2 * c) * 64
                        terms = [(nk[s][:, h, c * 128:c * 128 + kn], qq[s][:, h, :N], [bld[s]]),
                                 (self.identb[:kn, :kn], natab[:kn, 0, h * NA_TABW + c0:h * NA_TABW + c0 + 512], [self.bconst, bt]),
                                 (self.identb[:kn, :kn], narm[:kn, v, c * 512:(c + 1) * 512], [self.bconst, bt])]
                        chunks.append((kn, terms, nv[s][:kn, c, h * 64:(h + 1) * 64], [bld[s]]))
                for cc in range(2):
                    chunks.append((128, [(cnk[:, h, cc * 128:(cc + 1) * 128], qq[s][:, h, :N], [bc, bld[s]])], cv[:, cc, h * 64:(h + 1) * 64], [bc]))
                _attend(self, N, 64, chunks, pb, bpb, ystg[s][:, h, :N], bys[s][h], lt, blt)
            for hq in range(4):
                kvh = hq // 2
                chunks = []
                if lat:
                    for c in range(6):
                        if t == 0 and c == 0:
                            mk_ = swedge[:, 0, 0:512]
                        elif t == NT - 1 and c == 5:
                            mk_ = swedge[:, 0, 512:1024]
                        else:
                            mk_ = swtab[:, 0, (5 - c) * 128:(5 - c) * 128 + 512]
                        terms = [(sk[s][:, kvh, c * 128:(c + 1) * 128], qq[s][:, 4 + hq, :N], [bld[s]]),
                                 (self.identb[:, :], mk_, [self.bconst, bt])]
                        chunks.append((128, terms, sv[s][:, c, kvh * 64:(kvh + 1) * 64], [bld[s]]))
                for cc in range(2):
                    chunks.append((128, [(csk[:, kvh, cc * 128:(cc + 1) * 128], qq[s][:, 4 + hq, :N], [bc, bld[s]])], cv[:, cc, 256 + kvh * 64:256 + (kvh + 1) * 64], [bc]))
                _attend(self, N, 64, chunks, pb, bpb, ystg[s][:, 4 + hq, :N], bys[s][4 + hq], lt, blt, esink=esink, hsel=hq)
            P.dma("sp", yb[256:768, tok0:tok0 + N].rearrange("(h p) n -> p h n", p=64), ystg[s][:, :, :N], reads=bys[s])
            W = N + 16
            TT = lambda e, o, a, b_, op: e.tensor_tensor(o, a, b_, op)
            P.op("pool", lambda e: e.tensor_copy(X[:, :, :W], pin[s][:, :, :W]), reads=[bpin[s]], writes=[bpl[0]])
            P.op("pool", lambda e: TT(e, A2[:, :, 1:W], X[:, :, 0:W - 1], X[:, :, 1:W], ALU.add), reads=[bpl[0]], writes=[bpl[1]])
            P.op("pool", lambda e: TT(e, B4[:, :, 2:W - 1], A2[:, :, 1:W - 2], A2[:, :, 3:W], ALU.add), reads=[bpl[1]], writes=[bpl[2]])
            P.op("pool", lambda e: TT(e, B8[:, 4:W - 3], B4[:, 1, 2:W - 5], B4[:, 1, 6:W - 1], ALU.add), reads=[bpl[2]], writes=[bpl[3]])
            P.op("pool", lambda e: TT(e, SUM[64:128, 1, :N], B8[64:128, 4:4 + N], B8[64:128, 12:12 + N], ALU.add), reads=[bpl[3]], writes=[bpl[4]])
            P.op("pool", lambda e: e.tensor_copy(SUM[0:64, 1, :N], B8[0:64, 8:8 + N]), reads=[bpl[3]], writes=[bpl[4]])
            P.op("pool", lambda e: e.tensor_copy(SUM[0:64, 0, :N], A2[0:64, 0, 8:8 + N]), reads=[bpl[1]], writes=[bpl[4]])
            P.op("pool", lambda e: e.tensor_copy(SUM[64:128, 0, :N], B4[64:128, 0, 8:8 + N]), reads=[bpl[2]], writes=[bpl[4]])
            P.op("pool", lambda e: TT(e, SUM[:, :, :N], SUM[:, :, :N], icnt[s][:, :, :N], ALU.mult), reads=[bpl[4], bpin[s]], writes=[bpl[4]])
            P.op("pool", lambda e: TT(e, Yb[:, :, :N], SUM[:, :, :N], X[:, :, 8:8 + N], ALU.subtract), reads=[bpl[4], bpl[0]], writes=[bpl[5]])
            for c in range(2):
                P.op("pe", lambda e, c=c: e.matmul(self.ps[M_BANK][:, :N], pwbd[:, 0, c * 128:(c + 1) * 128], Yb[:, c, :N], start=True, stop=True),
                     reads=[bt, bpl[5]], writes=[self.bps[M_BANK]])
                P.op("act", lambda e, c=c: e.activation(pstg[s][:, c, :N], self.ps[M_BANK][:, :N], AF.Identity, scale=self.smallv[:, l, 16 + c:17 + c]),
                     reads=[self.bps[M_BANK], self.bsmall], writes=[bpstg[s]])
            P.dma("sp", yb[0:256, tok0:tok0 + N].rearrange("(c p) n -> p c n", p=128), pstg[s][:, :, :N], reads=[bpstg[s]])

        for ti, tl in enumerate(tiles):
            do_tile(ti, *tl)
        P.flush()


Builder.phase2_local = _phase2_local


def _na_tab(rpb):
    qc = np.arange(64)
    kc = np.arange(64)
    c0 = np.clip(qc - 8, 0, 48)
    colok = (kc[:, None] >= c0[None, :]) & (kc[:, None] <= c0[None, :] + 15)
    coff = np.clip(kc[:, None] - qc[None, :] + 15, 0, 30)
    out = np.full((128, 4, 22, 64), NEG, np.float32)
    for h in range(4):
        for j in range(22):
            for a in range(2):
                dr = 10 - j + a
                if abs(dr) > 7:
                    continue
                T = np.where(colok, rpb[h, dr + 7][coff], np.float32(NEG)).astype(np.float32)
                out[a * 64:(a + 1) * 64, h, j, :] = T
    return out.reshape(128, 4 * NA_TABW)


def _na_rowmask(q0, rows):
    m = np.full((128, 8, 8, 64), NEG, np.float32)
    for c in range(8):
        for a in range(2):
            kr = q0 - 4 + 2 * c + a
            for b in range(8):
                qr = q0 + b
                r0 = min(max(qr - 4, 0), rows - 8)
                if r0 <= kr <= r0 + 7 and not (c == 7 and a == 1):
                    m[a * 64:(a + 1) * 64, c, b, :] = 0.0
    return m.reshape(128, 8 * 512)


def _sw_tab():
    ki = np.arange(128)[:, None]
    qi = np.arange(128)[None, :]
    t = np.full((128, 9, 128), NEG, np.float32)
    for j in range(9):
        d = 5 - j
        if d == 0:
            t[:, j, :] = np.where(ki >= qi, 0.0, NEG)
        elif d == 1:
            t[:, j, :] = 0.0
        elif d == 2:
            t[:, j, :] = np.where(ki <= qi, 0.0, NEG)
    return t.reshape(128, 9 * 128)


def _pool_icnt(pos, n):
    out = np.zeros((128, 2, len(pos)), np.float32)
    for g, w in enumerate((2, 4, 8, 16)):
        lo = np.clip(pos - w // 2, 0, n)
        hi = np.clip(pos - w // 2 + w, 0, n)
        ic = (np.float32(1.0) / (hi - lo).astype(np.float32)).astype(np.float32)
        out[(g % 2) * 64:(g % 2) * 64 + 64, g // 2, :] = ic[None, :]
    return out.reshape(128, 2 * len(pos))


def host_shared2(inp):
    g = lambda k: np.asarray(inp[k], dtype=np.float32)
    sh = {}
    sh["na_tab"] = np.stack([_na_tab(g("na_rpb")[l]) for l in range(L)])
    sh["sw_tab"] = _sw_tab()
    sh["pool_icnt_c"] = _pool_icnt(np.arange(CTX), CTX)
    pw = g("pool_w")
    wbd = np.zeros((L, 128, 2, 128), np.float32)
    for l in range(L):
        for gi in range(4):
            c, gl = gi // 2, gi % 2
            wbd[l, gl * 64:(gl + 1) * 64, c, gl * 64:(gl + 1) * 64] = pw[l, gi]
    sh["pool_wbd"] = wbd.reshape(L, 128, 256)
    sh["w_uv"] = np.ascontiguousarray(g("mla_w_uv").reshape(L, 128, 256))
    sh["wg"] = np.ascontiguousarray(g("w_in")[:, :, OFF_GATE:])
    sh["w_branch"] = np.ascontiguousarray(g("w_branch").reshape(L, D, D))
    sh["w_out"] = g("w_out")
    sh["ffn_w1"] = g("ffn_w1")
    sh["ffn_w3"] = g("ffn_w3")
    sh["ffn_w2"] = g("ffn_w2")
    sh["final_gT"] = np.ascontiguousarray(g("final_norm_g").reshape(8, 128).T)
    return sh


PER_LAYER = {"ada_w": "ada_w%d", "w1": "w1_%d", "w_uq": "w_uq%d", "wqr": "wqr%d", "w_ukT": "w_ukT%d", "w_uv": "w_uv%d", "na_tab": "na_tab%d",
             "pool_wbd": "pool_wbd%d", "wg": "wg%d", "w_branch": "w_branch%d", "w_out": "w_out%d", "ffn_w1": "ffn_w1_%d", "ffn_w3": "ffn_w3_%d",
             "ffn_w2": "ffn_w2_%d"}


def split_layers(sh):
    out = {}
    for k, v in sh.items():
        if k in PER_LAYER:
            for l in range(L):
                out[PER_LAYER[k] % l] = np.ascontiguousarray(v[l])
        else:
            out[k] = v
    return out


PASS_KEYS = ("qs", "qlat", "qrope", "hx", "c_exk", "c_exv", "c_qs", "c_qlat", "c_qrope", "c_hx")
P1_OUTS = ("exk", "exv", "exvc", "qs", "qlat", "qrope", "hx", "c_exk", "c_exv", "c_exvc", "c_qs", "c_qlat", "c_qrope", "c_hx")


def run_launch(cfg, emit, ext_out, shared, percore, extra):
    avail = set(shared.keys()) | set(percore[0].keys())
    if extra is not None:
        avail |= set(extra[0].keys())
    B = Builder(cfg, None, ext_in=avail, ext_out=ext_out)
    with ExitStack() as st:
        B.setup(st)
        emit(B)
    maps = []
    for c in range(NCORES):
        m = {}
        for k in B.used_in:
            if extra is not None and k in extra[c]:
                m[k] = extra[c][k]
            elif k in percore[c]:
                m[k] = percore[c][k]
            else:
                m[k] = shared[k]
        maps.append(m)
    res = run_bass_kernel_spmd(B.nc, maps, core_ids=list(range(NCORES)))
    return [dict(r) for r in res.results]


def kernel_impl(inputs, NT):
    cfg = Cfg(NT)
    sh = host_shared(inputs)
    sh.update(host_shared2(inputs))
    sh = split_layers(sh)
    pcs = []
    for c in range(NCORES):
        pc = host_core(inputs, cfg, c)
        pc.update(host_core2(cfg, c))
        pcs.append(pc)

    def emitA(B):
        B.phase_mod(0)
        B.phase1(0, "xT")

    def emitB(B):
        B.phase_mod(0)
        B.phase_mod(1)
        B.phase2_mla(0, True)
        B.phase2_local(0, True)
        B.phase3(0, "xT", True)
        B.phase4(0, True, False)
        B.phase1(1, "x1")

    def emitC(B):
        B.phase_mod(1)
        B.phase2_mla(1, False)
        B.phase2_local(1, False)
        B.phase3(1, "x1", False)
        B.phase4(1, False, True)

    outA = run_launch(cfg, emitA, ["%s0" % k for k in P1_OUTS], sh, pcs, None)
    exB = [host_exchange(cfg, outA, 0, c) for c in range(NCORES)]
    del outA
    outB = run_launch(cfg, emitB, ["%s1" % k for k in P1_OUTS] + ["x1"], sh, pcs, exB)
    del exB
    exC = [host_exchange(cfg, outB, 1, c) for c in range(NCORES)]
    for c in range(NCORES):
        exC[c]["x1"] = outB[c]["x1"]
    del outB
    outC = run_launch(cfg, emitC, ["outT"], sh, pcs, exC)
    out = np.zeros((NCORES // CPB, cfg.S, D), np.float32)
    for c in range(NCORES):
        b, j = c // CPB, c % CPB
        out[b, j * cfg.TC:(j + 1) * cfg.TC, :] = np.asarray(outC[c]["outT"], dtype=np.float32).T
    return out


def kernel(**inputs):
    return kernel_impl(inputs, 8)


def host_core2(cfg, core):
    b, j = core // CPB, core % CPB
    TC, NT = cfg.TC, cfg.NT
    pc = {}
    gb0 = j * NT
    pc["na_rowmask"] = np.stack([_na_rowmask(8 * gb0, cfg.ROWS), _na_rowmask(8 * (gb0 + 1), cfg.ROWS) if NT > 2 else _na_rowmask(8 * gb0, cfg.ROWS),
                                 _na_rowmask(8 * (gb0 + NT - 1), cfg.ROWS)])
    tab = _sw_tab().reshape(128, 9, 128)
    e0 = tab[:, 5:9, :].reshape(128, 512).copy()
    e1 = tab[:, 0:4, :].reshape(128, 512).copy()
    if j == 0:
        e0[:] = NEG
    if j == CPB - 1:
        e1[:] = NEG
    pc["sw_edge"] = np.concatenate([e0, e1], axis=1)
    pc["pool_icnt"] = _pool_icnt(np.arange(j * TC, (j + 1) * TC), cfg.S)
    return pc


def host_exchange(cfg, outs, l, core):
    import ml_dtypes
    b, j = core // CPB, core % CPB
    TC, S = cfg.TC, cfg.S
    grp = [outs[b * CPB + i] for i in range(CPB)]
    exk = np.concatenate([g["exk%d" % l] for g in grp], axis=1)
    exv = np.concatenate([g["exv%d" % l] for g in grp], axis=0)
    exvc = np.concatenate([g["exvc%d" % l] for g in grp], axis=0)
    me = outs[core]
    ck = me["c_exk%d" % l]
    m = {}
    m["mk%d" % l] = np.ascontiguousarray(np.concatenate([np.concatenate([ck[R_CKV:R_CKV + 128], ck[R_KR:R_KR + 32]], 0),
                                                         np.concatenate([exk[R_CKV:R_CKV + 128], exk[R_KR:R_KR + 32]], 0)], axis=1))
    m["mv%d" % l] = np.ascontiguousarray(np.concatenate([me["c_exvc%d" % l], exvc], axis=0))

    def win_cols(a, lo, hi):
        out = np.zeros((a.shape[0], hi - lo), a.dtype)
        l0, h0 = max(lo, 0), min(hi, S)
        out[:, l0 - lo:h0 - lo] = a[:, l0:h0]
        return out

    def win_rows(a, lo, hi):
        return np.ascontiguousarray(win_cols(a.T, lo, hi).T)

    t0, t1 = j * TC, (j + 1) * TC
    m["nak%d" % l] = win_cols(exk[R_NAK:R_NAK + 256], t0 - 256, t1 + 256)
    m["nav%d" % l] = win_rows(exv[:, 0:256], t0 - 256, t1 + 256)
    m["swk%d" % l] = win_cols(exk[R_SWAK:R_SWAK + 128], t0 - 128, t1 + 128)
    m["swv%d" % l] = win_rows(exv[:, 256:384], t0 - 128, t1 + 128)
    m["poolin%d" % l] = win_cols(exk[R_POOL:R_POOL + 256], t0 - 8, t1 + 8)
    for k in PASS_KEYS:
        if "%s%d" % (k, l) in me:
            m["%s%d" % (k, l)] = me["%s%d" % (k, l)]
    return m


def _phase3(self, l, xname, with_ctx):
    P = self.P
    cfg = self.cfg
    TC, NT = cfg.TC, cfg.NT
    wgd = self.dram("wg%d" % l, [D, 4096], F32)
    wbrd = self.dram("w_branch%d" % l, [D, D], F32)
    wod = self.dram("w_out%d" % l, [D, D], F32)
    with ExitStack() as st:
        A = lambda name, shape, dt: self.alloc(st, name, shape, dt)
        wg = A("wg", [128, KC, 4096], BF16)
        wbr = A("wbr", [128, KC, D], BF16)
        wo = A("wo", [128, KC, D], BF16)
        bw = Buf("w3")
        self.cast_load(wbr, wbrd, D, [bw])
        self.cast_load(wg, wgd, 4096, [bw])
        self.cast_load(wo, wod, D, [bw])
        xt = A("xt", [128, KC, 512], F32)
        bx = bufs(KC, "xt")
        hh = [A("h%d" % i, [128, KC, 512], BF16) for i in range(2)]
        yt = [A("yt%d" % i, [128, KC, 512], BF16) for i in range(2)]
        bld = bufs(2, "ld3")
        sig = [A("sig%d" % i, [128, 512], F32) for i in range(2)]
        bsig = bufs(2, "sig")
        term = [A("term%d" % i, [128, 512], F32) for i in range(2)]
        bterm = bufs(2, "term")
        acc = A("acc", [128, 512], F32)
        bacc = Buf("acc")
        mg = A("mg", [128, KC, 512], BF16)
        bmg = bufs(KC, "mg")
        tiles = [("", 512, t * 512, 0) for t in range(NT)]
        if with_ctx:
            tiles = [("c_", CTX, 0, 1)] + tiles
        cnt = [0]

        def do_tile(ti, pre, N, tok0, cidx):
            s = ti % 2
            lat = pre == ""
            n_tot = TC if lat else CTX
            xsrc = self.dram(xname if lat else "ctxT", [D, n_tot], F32)
            xdst = self.dram(("xm%d" if lat else "xcm%d") % l, [D, n_tot], F32)
            hx = self.dram("%shx%d" % (pre, l), [D, n_tot], BF16)
            yb = self.dram("%syb%d" % (pre, l), [D, n_tot], BF16)
            P.dma("sp", hh[s][:, :, :N], hx[:, tok0:tok0 + N].rearrange("(k p) n -> p k n", p=128), writes=[bld[s]])
            P.dma("sp", yt[s][:, :, :N], yb[:, tok0:tok0 + N].rearrange("(k p) n -> p k n", p=128), writes=[bld[s]])
            P.dma("sp", xt[:, :, :N], xsrc[:, tok0:tok0 + N].rearrange("(k p) n -> p k n", p=128), writes=bx)
            h, y = hh[s], yt[s]
            for oc in range(KC):
                for i in range(4):
                    u = cnt[0] % 2
                    cnt[0] += 1
                    bg = self.proj(wg, bw, i * 1024 + oc * 128, 128, h, bld[s], N)
                    P.op("act", lambda e, bg=bg, u=u: e.activation(sig[u][:, :N], self.ps[bg][:, :N], AF.Sigmoid), reads=[self.bps[bg]], writes=[bsig[u]])
                    bb = self.bank()
                    for kk in range(2):
                        P.op("pe", lambda e, bb=bb, kk=kk, i=i, oc=oc: e.matmul(self.ps[bb][:, :N], wbr[:, i * 2 + kk, oc * 128:(oc + 1) * 128], y[:, i * 2 + kk, :N], start=(kk == 0), stop=(kk == 1)),
                             reads=[bw, bld[s]], writes=[self.bps[bb]])
                    if i == 0:
                        P.op("dve", lambda e, bb=bb, u=u: e.tensor_tensor(acc[:, :N], self.ps[bb][:, :N], sig[u][:, :N], ALU.mult), reads=[self.bps[bb], bsig[u]], writes=[bacc])
                    else:
                        P.op("dve", lambda e, bb=bb, u=u: e.tensor_tensor(term[u][:, :N], self.ps[bb][:, :N], sig[u][:, :N], ALU.mult), reads=[self.bps[bb], bsig[u]], writes=[bterm[u]])
                        if i < 3:
                            P.op("pool", lambda e, u=u: e.tensor_tensor(acc[:, :N], acc[:, :N], term[u][:, :N], ALU.add), reads=[bacc, bterm[u]], writes=[bacc])
                        else:
                            P.op("pool", lambda e, u=u, oc=oc: e.tensor_tensor(mg[:, oc, :N], acc[:, :N], term[u][:, :N], ALU.add), reads=[bacc, bterm[u]], writes=[bmg[oc]])
            for oc in range(KC):
                b = self.bank()
                for k in range(KC):
                    P.op("pe", lambda e, b=b, k=k, oc=oc: e.matmul(self.ps[b][:, :N], wo[:, k, oc * 128:(oc + 1) * 128], mg[:, k, :N], start=(k == 0), stop=(k == KC - 1)),
                         reads=[bw, bmg[k]], writes=[self.bps[b]])
                P.op("dve", lambda e, b=b, oc=oc: e.scalar_tensor_tensor(xt[:, oc, :N], self.ps[b][:, :N], self.mod[l][:, 16 + oc, cidx:cidx + 1], xt[:, oc, :N], ALU.mult, ALU.add),
                     reads=[self.bps[b], self.bmod[l], bx[oc]], writes=[bx[oc]])
            P.dma("sp", xdst[:, tok0:tok0 + N].rearrange("(k p) n -> p k n", p=128), xt[:, :, :N], reads=bx)

        for ti, tl in enumerate(tiles):
            do_tile(ti, *tl)
        P.flush()


def _phase4(self, l, with_ctx, last):
    P = self.P
    cfg = self.cfg
    TC = cfg.TC
    N = 256
    w1d = self.dram("ffn_w1_%d" % l, [D, FFN], F32)
    w3d = self.dram("ffn_w3_%d" % l, [D, FFN], F32)
    w2d = self.dram("ffn_w2_%d" % l, [FFN, D], F32)
    with ExitStack() as st:
        A = lambda name, shape, dt: self.alloc(st, name, shape, dt)
        w1 = A("fw1", [128, KC, FFN], BF16)
        w3 = A("fw3", [128, KC, FFN], BF16)
        w2 = A("fw2", [128, FC, D], BF16)
        bw = Buf("w4")
        self.cast_load(w1, w1d, FFN, [bw])
        self.cast_load(w3, w3d, FFN, [bw])
        self.cast_load(w2, w2d, D, [bw])
        xt = [A("xt%d" % i, [128, KC, N], F32) for i in range(2)]
        bx = [bufs(KC, "xt%d_" % i) for i in range(2)]
        sq = A("sq", [128, KC, N], F32)
        bsq = bufs(KC, "sq")
        r = A("r", [128, N], F32)
        br = Buf("r")
        h = A("h", [128, KC, N], BF16)
        bh = Buf("h")
        sg = A("sg", [128, FC, N], BF16)
        bsg = bufs(FC, "sg")
        sil = [A("sil%d" % i, [128, N], F32) for i in range(2)]
        bsil = bufs(2, "sil")
        fg = self.dram("final_gT", [128, KC], F32)
        fgt = A("fgt", [128, KC], F32)
        bfg = Buf("fg")
        if last:
            P.dma("sp", fgt[:], fg, writes=[bfg])
        tiles = [("", t * N, 0) for t in range(TC // N)]
        if with_ctx:
            tiles = [("c_", 0, 1)] + tiles

        def do_tile(ti, pre, tok0, cidx):
            s = ti % 2
            lat = pre == ""
            n_tot = TC if lat else CTX
            xsrc = self.dram(("xm%d" if lat else "xcm%d") % l, [D, n_tot], F32)
            if last:
                xdst = self.dram("outT", [D, n_tot], F32)
            else:
                xdst = self.dram(("x%d" if lat else "xcT%d") % (l + 1), [D, n_tot], F32)
            x = xt[s]
            P.dma("sp", x[:], xsrc[:, tok0:tok0 + N].rearrange("(k p) n -> p k n", p=128), writes=bx[s])
            bxall = Buf("xall")
            self.norm_mod(x, None, h, bh, sq, bsq, r, br, self.gsc2[l], self.mod[l][:, 24:32, :], cidx, N, l, bxl=bx[s])
            for fc in range(FC):
                u = fc % 2
                ba = self.proj(w1, bw, fc * 128, 128, h, bh, N)
                bb = self.proj(w3, bw, fc * 128, 128, h, bh, N)
                P.op("act", lambda e, ba=ba, u=u: e.activation(sil[u][:, :N], self.ps[ba][:, :N], AF.Silu), reads=[self.bps[ba]], writes=[bsil[u]])
                P.op("dve", lambda e, bb=bb, u=u, fc=fc: e.tensor_tensor(sg[:, fc, :N], self.ps[bb][:, :N], sil[u][:, :N], ALU.mult), reads=[self.bps[bb], bsil[u]], writes=[bsg[fc]])
            for oc in range(KC):
                b = self.bank()
                for fc in range(FC):
                    P.op("pe", lambda e, b=b, fc=fc, oc=oc: e.matmul(self.ps[b][:, :N], w2[:, fc, oc * 128:(oc + 1) * 128], sg[:, fc, :N], start=(fc == 0), stop=(fc == FC - 1)),
                         reads=[bw, bsg[fc]], writes=[self.bps[b]])
                P.op("dve", lambda e, b=b, oc=oc: e.scalar_tensor_tensor(x[:, oc, :N], self.ps[b][:, :N], self.mod[l][:, 40 + oc, cidx:cidx + 1], x[:, oc, :N], ALU.mult, ALU.add),
                     reads=[self.bps[b], self.bmod[l], bx[s][oc]], writes=[bx[s][oc]])
            if last:
                P.op("act", lambda e: e.activation(sq[:, :, :N], x[:, :, :N], AF.Square), reads=bx[s], writes=bsq)
                b = self.bank()
                for k in range(KC):
                    P.op("pe", lambda e, k=k, b=b: e.matmul(self.ps[b][:, :N], self.ones32[:], sq[:, k, :N], start=(k == 0), stop=(k == KC - 1)),
                         reads=[self.bconst, bsq[k]], writes=[self.bps[b]])
                P.op("act", lambda e, b=b: e.activation(r[:, :N], self.ps[b][:, :N], AF.Sqrt, bias=self.epsb[:], scale=1.0 / D), reads=[self.bps[b], self.bconst], writes=[br])
                P.op("dve", lambda e: e.reciprocal(r[:, :N], r[:, :N]), reads=[br], writes=[br])
                for k in range(KC):
                    P.op("dve", lambda e, k=k: e.scalar_tensor_tensor(x[:, k, :N], x[:, k, :N], fgt[:, k:k + 1], r[:, :N], ALU.mult, ALU.mult),
                         reads=[bx[s][k], bfg, br], writes=[bx[s][k]])
            P.dma("sp", xdst[:, tok0:tok0 + N].rearrange("(k p) n -> p k n", p=128), x[:], reads=bx[s])

        for ti, tl in enumerate(tiles):
            do_tile(ti, *tl)
        P.flush()


Builder.phase3 = _phase3
Builder.phase4 = _phase4
```
